# Optimizing a Trainium2 kernel written in Bass

```python
import jax, jax.numpy as jnp
from jax import lax
import numpy as np

D_MODEL = 2048
BATCH = 2
SEQ = 8192
DEPTH = 1

PLE_DIM = 256
ROPE_THETA = 10000.0
NORM_EPS = 1e-6
Q_BLOCK = 128
NEG = -1e30
A_HEADS = 16
A_KV_HEADS = 4
A_HEAD_DIM = 128
IDX_HEADS = 16
IDX_DIM = 64
TOPK_MAX = 256
B_HEADS = 16
B_Q_LORA = 512
B_KV_LORA = 256
B_NOPE = 128
B_ROPE = 64
B_V = 128
B_QK = B_NOPE + B_ROPE
N_BRANCH = 2
D_FF = ((8 * D_MODEL + 3 * 256 - 1) // (3 * 256)) * 256

IN_SIZES = (
    A_HEADS * A_HEAD_DIM,
    A_KV_HEADS * A_HEAD_DIM,
    A_KV_HEADS * A_HEAD_DIM,
    IDX_HEADS * IDX_DIM,
    IDX_DIM,
    IDX_HEADS,
    B_Q_LORA,
    B_KV_LORA,
    B_ROPE,
    N_BRANCH * D_MODEL,
)
N_IN = sum(IN_SIZES)
SPLIT_POINTS = [sum(IN_SIZES[:j]) for j in range(1, len(IN_SIZES))]

kernel_name = "hybrid_dsa_mla_gated_block"


def rms_norm(x, g):
    xf = x.astype(jnp.float32)
    y = xf * lax.rsqrt(jnp.mean(xf * xf, axis=-1, keepdims=True) + NORM_EPS)
    return (y * g.astype(jnp.float32)).astype(x.dtype)


def rope(x, pos):
    d = x.shape[-1]
    half = d // 2
    freqs = jnp.power(jnp.float32(ROPE_THETA), -jnp.arange(half, dtype=jnp.float32) * (2.0 / d))
    ang = pos.astype(jnp.float32)[..., None] * freqs
    cos = jnp.cos(ang)[:, :, None, :]
    sin = jnp.sin(ang)[:, :, None, :]
    xf = x.astype(jnp.float32)
    x1, x2 = xf[..., :half], xf[..., half:]
    return jnp.concatenate([x1 * cos - x2 * sin, x2 * cos + x1 * sin], axis=-1).astype(x.dtype)


def rope_tail(x, pos):
    return jnp.concatenate([x[..., :B_NOPE], rope(x[..., B_NOPE:], pos)], axis=-1)


def to_blocks(a):
    b, s = a.shape[0], a.shape[1]
    a = a.reshape((b, s // Q_BLOCK, Q_BLOCK) + a.shape[2:])
    return jnp.moveaxis(a, 1, 0)


def from_blocks(a):
    a = jnp.moveaxis(a, 0, 1)
    return a.reshape((a.shape[0], a.shape[1] * a.shape[2]) + a.shape[3:])


def dsa_attention(q, k, v, qi, ki, wi, pos, k_sel):
    b, s, ha, dh = q.shape
    hkv = k.shape[2]
    grp = ha // hkv
    scale = dh ** -0.5
    ki32 = ki.astype(jnp.float32)
    k32 = k.astype(jnp.float32)
    v32 = v.astype(jnp.float32)

    def block(args):
        qb, qib, wib, posb = args
        dots = jnp.einsum('bthd,bsd->bths', qib.astype(jnp.float32), ki32)
        idx_score = jnp.einsum('bth,bths->bts', wib.astype(jnp.float32), jax.nn.relu(dots))
        visible = pos[:, None, :] <= posb[:, :, None]
        idx_score = jnp.where(visible, idx_score, NEG)
        _, sel = lax.top_k(idx_score, k_sel)
        k_g = jax.vmap(lambda kb, ib: kb[ib])(k32, sel)
        v_g = jax.vmap(lambda vb, ib: vb[ib])(v32, sel)
        pos_g = jax.vmap(lambda pb, ib: pb[ib])(pos, sel)
        valid = pos_g <= posb[:, :, None]
        qg = qb.astype(jnp.float32).reshape(b, Q_BLOCK, hkv, grp, dh)
        logits = jnp.einsum('btngd,btjnd->btngj', qg, k_g) * scale
        logits = jnp.where(valid[:, :, None, None, :], logits, NEG)
        probs = jax.nn.softmax(logits, axis=-1)
        out = jnp.einsum('btngj,btjnd->btngd', probs, v_g)
        return out.reshape(b, Q_BLOCK, ha * dh).astype(q.dtype)

    outs = lax.map(block, (to_blocks(q), to_blocks(qi), to_blocks(wi), to_blocks(pos)))
    return from_blocks(outs)


def mla_attention(q, k, v, pos):
    b, s, h, dqk = q.shape
    scale = dqk ** -0.5
    k32 = k.astype(jnp.float32)
    v32 = v.astype(jnp.float32)

    def block(args):
        qb, posb = args
        logits = jnp.einsum('bthd,bshd->bhts', qb.astype(jnp.float32), k32) * scale
        mask = pos[:, None, None, :] <= posb[:, None, :, None]
        logits = jnp.where(mask, logits, NEG)
        probs = jax.nn.softmax(logits, axis=-1)
        out = jnp.einsum('bhts,bshd->bthd', probs, v32)
        return out.reshape(b, Q_BLOCK, h * v.shape[-1]).astype(q.dtype)

    outs = lax.map(block, (to_blocks(q), to_blocks(pos)))
    return from_blocks(outs)


def setup_inputs(seed: int = 0) -> dict:
    key = jax.random.key(seed)
    ks = iter(jax.random.split(key, 32))

    def w(shape, fan_in):
        return jax.random.normal(next(ks), shape, jnp.float32) * (fan_in ** -0.5)

    def gain(shape):
        return 1.0 + 0.05 * jax.random.normal(next(ks), shape, jnp.float32)

    x = jax.random.normal(next(ks), (BATCH, SEQ, D_MODEL), jnp.float32)
    p = jax.random.normal(next(ks), (DEPTH, BATCH, SEQ, PLE_DIM), jnp.float32)
    positions = jnp.broadcast_to(jnp.arange(SEQ, dtype=jnp.int32), (BATCH, SEQ))
    return {
        "x": x,
        "p": p,
        "positions": positions,
        "g_mix_norm": gain((DEPTH, D_MODEL)),
        "w_in": w((DEPTH, D_MODEL, N_IN), D_MODEL),
        "g_qa": gain((DEPTH, A_HEAD_DIM)),
        "g_ka": gain((DEPTH, A_HEAD_DIM)),
        "g_cq": gain((DEPTH, B_Q_LORA)),
        "w_uq": w((DEPTH, B_Q_LORA, B_HEADS * B_QK), B_Q_LORA),
        "g_ckv": gain((DEPTH, B_KV_LORA)),
        "w_ukv": w((DEPTH, B_KV_LORA, B_HEADS * (B_NOPE + B_V)), B_KV_LORA),
        "g_qb": gain((DEPTH, B_QK)),
        "g_kb": gain((DEPTH, B_QK)),
        "w_out_a": w((DEPTH, A_HEADS * A_HEAD_DIM, D_MODEL), A_HEADS * A_HEAD_DIM),
        "w_out_b": w((DEPTH, B_HEADS * B_V, D_MODEL), B_HEADS * B_V),
        "w_o": w((DEPTH, D_MODEL, D_MODEL), D_MODEL),
        "g_ffn_norm": gain((DEPTH, D_MODEL)),
        "w_ffn_gate": w((DEPTH, D_MODEL, D_FF), D_MODEL),
        "w_ffn_up": w((DEPTH, D_MODEL, D_FF), D_MODEL),
        "w_ffn_down": w((DEPTH, D_FF, D_MODEL), D_FF),
        "g_ple_norm": gain((DEPTH, D_MODEL)),
        "w_ple_gate": w((DEPTH, D_MODEL, D_MODEL), D_MODEL),
        "w_ple_proj": w((DEPTH, PLE_DIM, D_MODEL), PLE_DIM),
    }


def reference(x, p, positions, g_mix_norm, w_in, g_qa, g_ka, g_cq, w_uq, g_ckv, w_ukv,
              g_qb, g_kb, w_out_a, w_out_b, w_o, g_ffn_norm, w_ffn_gate, w_ffn_up,
              w_ffn_down, g_ple_norm, w_ple_gate, w_ple_proj):
    b, s, _ = x.shape
    k_sel = min(TOPK_MAX, s // 4)
    for i in range(DEPTH):
        h = rms_norm(x, g_mix_norm[i])
        proj = h @ w_in[i]
        qa, ka, va, qi, ki, wi, cq, ckv, kr, gates = jnp.split(proj, SPLIT_POINTS, axis=-1)

        qa = rope(rms_norm(qa.reshape(b, s, A_HEADS, A_HEAD_DIM), g_qa[i]), positions)
        ka = rope(rms_norm(ka.reshape(b, s, A_KV_HEADS, A_HEAD_DIM), g_ka[i]), positions)
        va = va.reshape(b, s, A_KV_HEADS, A_HEAD_DIM)
        qi = rope(qi.reshape(b, s, IDX_HEADS, IDX_DIM), positions)
        ki = rope(ki[:, :, None, :], positions)[:, :, 0, :]
        wi = wi * ((IDX_HEADS ** -0.5) * (IDX_DIM ** -0.5))
        y_a = dsa_attention(qa, ka, va, qi, ki, wi, positions, k_sel)

        qb = (rms_norm(cq, g_cq[i]) @ w_uq[i]).reshape(b, s, B_HEADS, B_QK)
        kv = (rms_norm(ckv, g_ckv[i]) @ w_ukv[i]).reshape(b, s, B_HEADS, B_NOPE + B_V)
        k_nope, vb = kv[..., :B_NOPE], kv[..., B_NOPE:]
        k_pe = jnp.broadcast_to(kr[:, :, None, :], (b, s, B_HEADS, B_ROPE))
        kb = jnp.concatenate([k_nope, k_pe], axis=-1)
        qb = rope_tail(rms_norm(qb, g_qb[i]), positions)
        kb = rope_tail(rms_norm(kb, g_kb[i]), positions)
        y_b = mla_attention(qb, kb, vb, positions)

        gate_a, gate_b = jnp.split(jax.nn.sigmoid(gates), N_BRANCH, axis=-1)
        mixed = gate_a * (y_a @ w_out_a[i]) + gate_b * (y_b @ w_out_b[i])
        x = x + mixed @ w_o[i]

        h2 = rms_norm(x, g_ffn_norm[i])
        x = x + (jax.nn.silu(h2 @ w_ffn_gate[i]) * (h2 @ w_ffn_up[i])) @ w_ffn_down[i]

        ple_gate = jax.nn.sigmoid(rms_norm(x, g_ple_norm[i]) @ w_ple_gate[i])
        x = x + ple_gate * (p[i] @ w_ple_proj[i])
    return x
```

```python
import math
from contextlib import ExitStack

import numpy as np
import ml_dtypes

import concourse.bass as bass
import concourse.mybir as mybir
from concourse.bass_utils import run_bass_kernel_spmd

F32 = mybir.dt.float32
BF16 = mybir.dt.bfloat16
I32 = mybir.dt.int32
ALU = mybir.AluOpType
AF = mybir.ActivationFunctionType
AX = mybir.AxisListType

P = 128
TG = 512
EPS = 1e-6
THETA = 10000.0
NEG_BIG = 1.0e30
TWO_PI = 2.0 * math.pi


class Cfg:
    def __init__(self, D=2048, S=8192, HA=16, HKV=4, HI=16, HB=16, QL=512, KVL=256,
                 DFF=5632, PLE=256, TOPK=256, NBIS=20):
        self.D, self.S, self.HA, self.HKV, self.HI, self.HB = D, S, HA, HKV, HI, HB
        self.QL, self.KVL, self.DFF, self.PLE, self.TOPK = QL, KVL, DFF, PLE, TOPK
        self.NBIS = NBIS
        self.DH, self.DI, self.NOPE, self.ROPE, self.BV = 128, 64, 128, 64, 128
        self.GRP = HA // HKV
        assert self.GRP == 4
        self.DC = D // P
        self.NB = S // P
        self.NQB = self.NB // 4
        self.TQ = self.NQB * P
        self.QC = QL // P
        self.KC = KVL // P
        self.FC = DFF // P
        self.PC = PLE // P
        self.KSEL = min(TOPK, S // 4)
        sizes = [HA * 128, HKV * 128, HKV * 128, HI * 64, 64, HI, QL, KVL, 64, 2 * D]
        offs = [0]
        for s in sizes:
            offs.append(offs[-1] + s)
        (self.o_qa, self.o_ka, self.o_va, self.o_qi, self.o_ki, self.o_wi, self.o_cq,
         self.o_ckv, self.o_kr, self.o_g, self.NIN) = offs
        assert self.TQ % TG == 0 and HI % 2 == 0


class Reg:
    __slots__ = ("name", "w", "r")

    def __init__(self, name):
        self.name = name
        self.w = None
        self.r = []


class _Rec:
    def __init__(self):
        self.call = None

    def __getattr__(self, name):
        def f(*a, **k):
            assert self.call is None
            self.call = (name, a, k)
            return self
        return f


class Sched:
    COMPUTE = ("pe", "act", "dve", "pool")
    NDMA = 8

    def __init__(self):
        self.prog = {k: [] for k in ("pe", "act", "dve", "pool", "sp")}
        self.cnt = {}
        self.seen = {k: {} for k in self.prog}
        self.dma_rr = {"sp": 0, "pool": 0}
        self.nobar = set()
        for k in self.COMPUTE:
            self.cnt[k] = 0
        for q in ("sp", "pool"):
            for i in range(self.NDMA):
                self.cnt[f"{q}_d{i}"] = 0
        for i in range(4):
            self.cnt[f"wc_d{i}"] = 0
            self.nobar.add(f"wc_d{i}")
        self.wc_rr = 0

    def _deps(self, eng, reads, writes):
        deps = {}

        def add(tok):
            if tok is None:
                return
            k, v = tok
            if k == "pe" and eng == "pe":
                return
            if deps.get(k, 0) < v:
                deps[k] = v
        for r in reads:
            add(r.w)
        for w in writes:
            add(w.w)
            for t in w.r:
                add(t)
        return deps

    def _emit(self, eng, deps, fn, semkey, inc):
        seen = self.seen[eng]
        waits = []
        for k, v in deps.items():
            if seen.get(k, 0) < v:
                seen[k] = v
                waits.append((k, v))
        self.cnt[semkey] += inc
        tok = (semkey, self.cnt[semkey])
        rec = _Rec()
        fn(rec)
        assert rec.call is not None
        self.prog[eng].append((waits, rec.call, semkey, inc))
        return tok

    def op(self, eng, fn, reads=(), writes=()):
        deps = self._deps(eng, reads, writes)
        tok = self._emit(eng, deps, fn, eng, 1)
        for r in reads:
            r.r.append(tok)
        for w in writes:
            w.w = tok
            w.r = []
        return tok

    def dma(self, q, fn, reads=(), writes=(), wcast=False):
        deps = self._deps(q, reads, writes)
        if wcast:
            semkey = f"wc_d{self.wc_rr % 4}"
            self.wc_rr += 1
        else:
            semkey = f"{q}_d{self.dma_rr[q] % self.NDMA}"
            self.dma_rr[q] += 1
        prev = self.cnt[semkey]
        if prev > 0 and deps.get(semkey, 0) < prev:
            deps[semkey] = prev
        tok = self._emit(q, deps, fn, semkey, 16)
        for r in reads:
            r.r.append(tok)
        for w in writes:
            w.w = tok
            w.r = []
        return tok

    def barrier(self):
        for eng in self.prog:
            seen = self.seen[eng]
            waits = []
            for k, v in self.cnt.items():
                if k in self.nobar or k == eng:
                    continue
                if v > 0 and seen.get(k, 0) < v:
                    seen[k] = v
                    waits.append((k, v))
            if waits:
                self.prog[eng].append((waits, None, None, 0))

    def final_wait(self):
        seen = self.seen["sp"]
        waits = []
        for k, v in self.cnt.items():
            if v > 0 and seen.get(k, 0) < v:
                seen[k] = v
                waits.append((k, v))
        if waits:
            self.prog["sp"].append((waits, None, None, 0))

    def replay(self, eng, handle, sems):
        for waits, fn, semkey, inc in self.prog[eng]:
            for k, v in waits:
                handle.wait_ge(sems[k], v)
            if fn is not None:
                name, a, k = fn
                getattr(handle, name)(*a, **k).then_inc(sems[semkey], inc)


class Arena:
    def __init__(self, big, nwords):
        self.big = big
        self.n = nwords
        self.off = 0
        self.marks = []

    def alloc(self, name, free_shape, dtype, parts=P):
        nel = 1
        for s in free_shape:
            nel *= s
        words = nel if dtype in (F32, I32) else (nel + 1) // 2
        words = (words + 1) // 2 * 2
        assert self.off + words <= self.n, f"SBUF arena overflow at {name}: {self.off}+{words}>{self.n}"
        v = self.big[:, self.off:self.off + words]
        self.off += words
        if dtype != F32:
            v = v.bitcast(dtype)
        v = v[:, 0:nel]
        if len(free_shape) == 2:
            v = v.rearrange("p (a b) -> p a b", b=free_shape[1])
        elif len(free_shape) == 3:
            v = v.rearrange("p (a b c) -> p a b c", b=free_shape[1], c=free_shape[2])
        return v, Reg(name)

    def mark(self):
        self.marks.append(self.off)

    def release(self):
        self.off = self.marks.pop()


def _consts_layout(cfg):
    names = [("freqA", 1), ("freqI", 1), ("g_mix", cfg.DC), ("g_qa", 1), ("g_ka", 1),
             ("g_cq", cfg.QC), ("g_ckv", cfg.KC), ("g_qbn", 1), ("g_qbr", 1), ("g_kbn", 1),
             ("g_kbr", 1), ("g_ffn", cfg.DC), ("g_ple", cfg.DC)]
    lay = {}
    o = 0
    for n, w in names:
        lay[n] = (o, w)
        o += w
    return lay, o


def build_program(cfg, debug_outputs=False):
    c = cfg
    nc = bass.Bass("TRN2", target_bir_lowering=False)
    S_ = Sched()
    D, S, TQ, DC, NB, NQB = c.D, c.S, c.TQ, c.DC, c.NB, c.NQB
    NKG = S // TG
    NQG = TQ // TG
    kindI = "Internal"

    def din(name, shape, dt):
        return nc.dram_tensor(name, list(shape), dt, kind="ExternalInput").ap()

    def dscr(name, shape, dt):
        if debug_outputs:
            return nc.dram_tensor(name, list(shape), dt, kind="ExternalOutput").ap()
        return nc.dram_tensor(name, list(shape), dt).ap()

    clay, ncc = _consts_layout(c)
    xT_all = din("xT_all", [D, S], F32)
    xT_own = din("xT_own", [D, TQ], F32)
    pT_own = din("pT_own", [c.PLE, TQ], F32)
    pos_all = din("pos_all", [1, S], I32)
    pos_own = din("pos_own", [1, TQ], I32)
    pos_allT = din("pos_allT", [P, NB], I32)
    pos_ownT = din("pos_ownT", [P, NQB], I32)
    consts_in = din("consts", [P, ncc], F32)
    ident_in = din("ident", [P, P], BF16)
    ones_in = din("ones", [P, P], BF16)
    permA_in = din("permA", [P, P], BF16)
    permI_in = din("permI", [P, P], BF16)
    identf_in = din("identf", [P, P], F32)
    w_in = din("w_in", [D, c.NIN], F32)
    w_uq = din("w_uq", [c.QL, c.HB * 192], F32)
    w_ukv = din("w_ukv", [c.KVL, c.HB * 256], F32)
    w_oa = din("w_out_a", [c.HA * 128, D], F32)
    w_ob = din("w_out_b", [c.HB * 128, D], F32)
    w_o = din("w_o", [D, D], F32)
    w_fg = din("w_ffn_gate", [D, c.DFF], F32)
    w_fu = din("w_ffn_up", [D, c.DFF], F32)
    w_fd = din("w_ffn_down", [c.DFF, D], F32)
    w_pg = din("w_ple_gate", [D, D], F32)
    w_pp = din("w_ple_proj", [c.PLE, D], F32)
    outT = nc.dram_tensor("outT", [D, TQ], F32, kind="ExternalOutput").ap()

    wb = {}
    for nm, src in (("w_in", w_in), ("w_ukv", w_ukv), ("w_uq", w_uq), ("w_oa", w_oa), ("w_ob", w_ob),
                    ("w_o", w_o), ("w_fg", w_fg), ("w_fu", w_fu), ("w_fd", w_fd), ("w_pg", w_pg),
                    ("w_pp", w_pp), ("pT", pT_own)):
        wb[nm] = (dscr(nm + "_bf", src.shape, BF16), src, Reg(nm + "_bf"))
    posf_all = dscr("posf_all", [1, S], F32)
    posf_own = dscr("posf_own", [1, TQ], F32)
    tabs = {}
    for nm, n in (("cosA_k", S), ("sinA_k", S), ("cosI_k", S), ("sinI_k", S),
                  ("cosA_q", TQ), ("sinA_q", TQ), ("cosI_q", TQ), ("sinI_q", TQ)):
        tabs[nm] = dscr(nm, [P, n], F32)
    kaT = dscr("kaT", [c.HKV, P, S], BF16)
    va_tok = dscr("va_tok", [c.HKV, P, NB, 130], BF16)
    kiT2 = dscr("kiT2", [P, S], BF16)
    ckvnT = dscr("ckvnT", [c.KC, P, S], BF16)
    krgrT = dscr("krgrT", [64, S], BF16)
    kr2T = dscr("kr2T", [64, S], BF16)
    qaT = dscr("qaT", [c.HA, P, TQ], BF16)
    qiT = dscr("qiT", [c.HI // 2, P, TQ], BF16)
    wi_tok = dscr("wi_tok", [TQ, c.HI], F32)
    qbnT = dscr("qbnT", [c.HB, P, TQ], BF16)
    qbrT = dscr("qbrT", [c.HB, 64, TQ], BF16)
    gatesT = dscr("gatesT", [2 * DC, P, TQ], BF16)
    maskT = dscr("maskT", [NQB, P, NB * P], BF16)
    yaT = dscr("yaT", [c.HA, P, TQ], BF16)
    ybT = dscr("ybT", [c.HB, P, TQ], BF16)
    KnT_all = dscr("KnT_all", [c.HB, P, S], BF16)
    KrT_all = dscr("KrT_all", [c.HB, 64, S], BF16)
    V_all = dscr("V_all", [c.HB, P, NB, 130], BF16)

    NW = 45000
    es = ExitStack()
    big = es.enter_context(nc.sbuf_tensor("arena", [P, NW], F32))
    A = Arena(big, NW)
    psb = []
    psr = []
    for i in range(8):
        t = es.enter_context(nc.psum_tensor(f"ps{i}", [P, 512], F32))
        psb.append(t)
        psr.append(Reg(f"ps{i}"))

    def PS(i):
        return psb[i][:, :]

    op, dma = S_.op, S_.dma

    def cast_rows(nm, r0, r1):
        dst, src, reg = wb[nm]
        dma("pool", lambda e, d=dst, s=src, a=r0, b=r1: e.dma_start(out=d[a:b, :], in_=s[a:b, :]),
            writes=[reg], wcast=True)

    def cast_all(nm):
        dst, src, reg = wb[nm]
        R = src.shape[0]
        step = 512
        for r0 in range(0, R, step):
            cast_rows(nm, r0, min(R, r0 + step))

    for nm in ("w_in", "w_ukv", "w_uq", "pT", "w_oa", "w_ob", "w_o", "w_fg", "w_fu", "w_fd", "w_pg", "w_pp"):
        cast_all(nm)

    cst, cst_r = A.alloc("consts", [ncc], F32)
    ident, ident_r = A.alloc("ident", [P], BF16)
    ones, ones_r = A.alloc("ones", [P], BF16)
    permA, permA_r = A.alloc("permA", [P], BF16)
    permI, permI_r = A.alloc("permI", [P], BF16)
    identf, identf_r = A.alloc("identf", [P], F32)
    dma("sp", lambda e: e.dma_start(out=cst, in_=consts_in), writes=[cst_r])
    dma("sp", lambda e: e.dma_start(out=ident, in_=ident_in), writes=[ident_r])
    dma("sp", lambda e: e.dma_start(out=ones, in_=ones_in), writes=[ones_r])
    dma("sp", lambda e: e.dma_start(out=permA, in_=permA_in), writes=[permA_r])
    dma("sp", lambda e: e.dma_start(out=permI, in_=permI_in), writes=[permI_r])
    dma("sp", lambda e: e.dma_start(out=identf, in_=identf_in), writes=[identf_r])
    CR = [cst_r]

    def C(name, j=0, rows=P):
        o, w = clay[name]
        return cst[0:rows, o + j:o + j + 1]

    def mm(out, lhsT, rhs, start, stop, reads, writes):
        return op("pe", lambda e: e.matmul(out, lhsT, rhs, start=start, stop=stop), reads=reads, writes=writes)

    def rstd_ps(ps_ap, ps_reg, dim, tmp, tmp_r, out, out_r):
        op("act", lambda e: e.activation(tmp, ps_ap, AF.Ln, bias=C_eps[0:ps_ap.shape[0], :], scale=1.0 / dim),
           reads=[ps_reg, eps_r], writes=[tmp_r])
        op("act", lambda e: e.activation(out, tmp, AF.Exp, scale=-0.5), reads=[tmp_r], writes=[out_r])

    C_eps, eps_r = A.alloc("eps", [1], F32)
    op("dve", lambda e: e.memset(C_eps, EPS), writes=[eps_r])

    A.mark()
    def p0_tables(pos_src, posf_dst, n, tA_cos, tA_sin, tI_cos, tI_sin):
        CH = 2048 if n >= 2048 else n
        pi_, pi_r = A.alloc("pos_i", [n], I32, parts=1)
        pf_, pf_r = A.alloc("pos_f", [n], F32, parts=1)
        dma("sp", lambda e: e.dma_start(out=pi_[0:1, :], in_=pos_src), writes=[pi_r])
        op("dve", lambda e: e.tensor_copy(pf_[0:1, :], pi_[0:1, :]), reads=[pi_r], writes=[pf_r])
        dma("sp", lambda e: e.dma_start(out=posf_dst, in_=pf_[0:1, :]), reads=[pf_r])
        S_.barrier()
        pb, pb_r = A.alloc("pos_bc", [CH], F32)
        ang, ang_r = A.alloc("ang", [CH], F32)
        kf, kf_r = A.alloc("kf", [CH], F32)
        ki_, ki_r = A.alloc("ki", [CH], I32)
        w1, w1_r = A.alloc("w1", [CH], F32)
        sn, sn_r = A.alloc("sn", [CH], F32)
        cs, cs_r = A.alloc("cs", [CH], F32)
        c1 = 6.28125
        c2 = 4058 * 2.0 ** -21
        c3 = TWO_PI - c1 - c2
        for kind, fname, tcos, tsin in (("A", "freqA", tA_cos, tA_sin), ("I", "freqI", tI_cos, tI_sin)):
            for c0 in range(0, n, CH):
                dma("sp", lambda e, c0=c0: e.dma_start(out=pb, in_=posf_dst[0:1, c0:c0 + CH].partition_broadcast(P)),
                    writes=[pb_r])
                op("dve", lambda e, fname=fname: e.tensor_scalar(ang, pb, C(fname), None, ALU.mult),
                   reads=[pb_r] + CR, writes=[ang_r])
                op("dve", lambda e: e.tensor_scalar(kf, ang, 1.0 / TWO_PI, None, ALU.mult), reads=[ang_r], writes=[kf_r])
                op("dve", lambda e: e.tensor_copy(ki_, kf), reads=[kf_r], writes=[ki_r])
                op("dve", lambda e: e.tensor_copy(kf, ki_), reads=[ki_r], writes=[kf_r])
                for cc in (c1, c2, c3):
                    op("dve", lambda e, cc=cc: e.scalar_tensor_tensor(ang, kf, -cc, ang, ALU.mult, ALU.add),
                       reads=[kf_r, ang_r], writes=[ang_r])
                op("dve", lambda e: e.tensor_scalar(w1, ang, math.pi, -TWO_PI, ALU.is_gt, ALU.mult),
                   reads=[ang_r], writes=[w1_r])
                op("dve", lambda e: e.tensor_tensor(ang, ang, w1, ALU.add), reads=[ang_r, w1_r], writes=[ang_r])
                op("dve", lambda e: e.tensor_scalar(w1, ang, -math.pi, TWO_PI, ALU.is_lt, ALU.mult),
                   reads=[ang_r], writes=[w1_r])
                op("dve", lambda e: e.tensor_tensor(ang, ang, w1, ALU.add), reads=[ang_r, w1_r], writes=[ang_r])
                op("dve", lambda e: e.tensor_scalar(ang, ang, math.pi, -math.pi, ALU.min, ALU.max),
                   reads=[ang_r], writes=[ang_r])
                op("act", lambda e: e.activation(sn, ang, AF.Sin), reads=[ang_r], writes=[sn_r])
                op("act", lambda e: e.activation(cs, ang, AF.Sin, scale=0.5), reads=[ang_r], writes=[cs_r])
                op("dve", lambda e: e.tensor_tensor(cs, cs, cs, ALU.mult), reads=[cs_r], writes=[cs_r])
                op("dve", lambda e: e.tensor_scalar(cs, cs, -2.0, 1.0, ALU.mult, ALU.add), reads=[cs_r], writes=[cs_r])
                dma("sp", lambda e, c0=c0, tsin=tsin: e.dma_start(out=tsin[:, c0:c0 + CH], in_=sn), reads=[sn_r])
                dma("sp", lambda e, c0=c0, tcos=tcos: e.dma_start(out=tcos[:, c0:c0 + CH], in_=cs), reads=[cs_r])

    A.mark()
    p0_tables(pos_all, posf_all, S, tabs["cosA_k"], tabs["sinA_k"], tabs["cosI_k"], tabs["sinI_k"])
    A.release()
    S_.barrier()
    A.mark()
    p0_tables(pos_own, posf_own, TQ, tabs["cosA_q"], tabs["sinA_q"], tabs["cosI_q"], tabs["sinI_q"])
    A.release()
    S_.barrier()

    def load_x_group(xsrc, t0, X, X_r):
        dma("sp", lambda e: e.dma_start(out=X, in_=xsrc.rearrange("(c p) t -> p c t", p=P)[:, :, t0:t0 + TG]),
            writes=[X_r])

    def prep_x(X, X_r, gname, sq, sq_r, xg, xg_r, rstd, rstd_r, tmp, tmp_r, psi):
        op("act", lambda e: e.activation(sq, X, AF.Square), reads=[X_r], writes=[sq_r])
        for cc in range(DC):
            op("dve", lambda e, cc=cc: e.tensor_scalar(xg[:, cc, :], X[:, cc, :], C(gname, cc), None, ALU.mult),
               reads=[X_r] + CR, writes=[xg_r])
        for cc in range(DC):
            mm(PS(psi), ones, sq[:, cc, :], cc == 0, cc == DC - 1, [ones_r, sq_r], [psr[psi]])
        rstd_ps(PS(psi), psr[psi], float(D), tmp, tmp_r, rstd, rstd_r)

    def rope_apply(z, z_r, rows, perm, perm_r, cosT, sinT, tab_rs, psi, t1, t1_r, out, out_r):
        mm(PS(psi)[0:rows, :], perm[0:rows, 0:rows], z[0:rows, :], True, True, [perm_r, z_r], [psr[psi]])
        op("dve", lambda e: e.tensor_tensor(t1[0:rows, :], z[0:rows, :], cosT[0:rows, :], ALU.mult),
           reads=[z_r] + list(tab_rs), writes=[t1_r])
        op("dve", lambda e: e.tensor_tensor(PS(psi)[0:rows, :], PS(psi)[0:rows, :], sinT[0:rows, :], ALU.mult),
           reads=[psr[psi]] + list(tab_rs), writes=[psr[psi]])
        op("dve", lambda e: e.tensor_tensor(out[0:rows, :], t1[0:rows, :], PS(psi)[0:rows, :], ALU.add),
           reads=[t1_r, psr[psi]], writes=[out_r])

    class Rot:
        def __init__(self, items):
            self.items = items
            self.i = 0

        def next(self):
            v = self.items[self.i % len(self.items)]
            self.i += 1
            return v

    def run_pipeline(items):
        n = len(items)
        maxst = max(max(len(it) for it in items), 1)
        for step in range(n + maxst - 1):
            for st in range(maxst):
                i = step - st
                if 0 <= i < n and st < len(items[i]):
                    items[i][st]()

    class ProjCtx:
        def __init__(self):
            self.ys = Rot([A.alloc(f"y{i}", [TG], F32) for i in range(3)])
            self.zs = Rot([A.alloc(f"z{i}", [TG], BF16) for i in range(4)])
            self.sqs = Rot([A.alloc(f"sqy{i}", [TG], BF16) for i in range(3)])
            self.rqs = Rot([A.alloc(f"rq{i}", [TG], F32) for i in range(2)])
            self.tmps = Rot([A.alloc(f"tmp{i}", [TG], F32) for i in range(2)])
            self.t1s = Rot([A.alloc(f"t1{i}", [TG], F32) for i in range(2)])
            self.obs = Rot([A.alloc(f"ob{i}", [TG], BF16) for i in range(4)])
            zr = [A.alloc(f"zr{i}", [TG], BF16) for i in range(2)]
            for t, r in zr:
                op("dve", lambda e: e.memset(t, 0.0), writes=[r])
            self.zrs = Rot(zr)
            self.pb = Rot([0, 1, 2, 3])
            self.ab = Rot([4, 5])
            self.rb = Rot([6, 7])
            self.sq4 = Rot([A.alloc(f"sq4{i}", [2, TG], BF16) for i in range(2)])

        def prep(self, X, X_r, gname, xg, xg_r, rstd, rstd_r):
            psi = self.ab.next()
            for c0 in range(0, DC, 2):
                nck = min(2, DC - c0)
                sq, sq_r = self.sq4.next()
                op("act", lambda e: e.activation(sq[:, 0:nck, :], X[:, c0:c0 + nck, :], AF.Square), reads=[X_r], writes=[sq_r])
                for k in range(nck):
                    cc = c0 + k
                    mm(PS(psi), ones, sq[:, k, :], cc == 0, cc == DC - 1, [ones_r, sq_r], [psr[psi]])
            for cc in range(DC):
                op("dve", lambda e: e.tensor_scalar(xg[:, cc, :], X[:, cc, :], C(gname, cc), None, ALU.mult),
                   reads=[X_r] + CR, writes=[xg_r])
            tmp, tmp_r = self.tmps.next()
            rstd_ps(PS(psi), psr[psi], float(D), tmp, tmp_r, rstd, rstd_r)

        def proj(self, W, W_r, col0, M, xg, xg_r, psi):
            for cc in range(DC):
                mm(PS(psi)[0:M, :], W[:, cc, col0:col0 + M], xg[:, cc, :], cc == 0, cc == DC - 1, [W_r, xg_r], [psr[psi]])

        def rope_stage(self, z, z_r, perm, perm_r, cosT, sinT, tab_rs, rows, dst):
            rb = self.rb.next()
            t1, t1_r = self.t1s.next()
            ob, ob_r = self.obs.next()
            mm(PS(rb), perm, z, True, True, [perm_r, z_r], [psr[rb]])
            op("dve", lambda e: e.tensor_tensor(t1, z, cosT, ALU.mult), reads=[z_r] + list(tab_rs), writes=[t1_r])
            op("dve", lambda e: e.tensor_tensor(PS(rb), PS(rb), sinT, ALU.mult), reads=[psr[rb]] + list(tab_rs), writes=[psr[rb]])
            op("dve", lambda e: e.tensor_tensor(ob, t1, PS(rb), ALU.add), reads=[t1_r, psr[rb]], writes=[ob_r])
            dma("sp", lambda e: e.dma_start(out=dst, in_=ob[0:rows, :]), reads=[ob_r])

        def item_normrope(self, W, W_r, col0, gname, xg, xg_r, rstd, rstd_r, perm, perm_r, cosT, sinT, tab_rs, dst):
            pb, ab = self.pb.next(), self.ab.next()
            y, y_r = self.ys.next()
            z, z_r = self.zs.next()
            sqy, sqy_r = self.sqs.next()
            rq, rq_r = self.rqs.next()
            tmp, tmp_r = self.tmps.next()

            def s0():
                self.proj(W, W_r, col0, 128, xg, xg_r, pb)
                op("dve", lambda e: e.tensor_tensor(y, PS(pb), rstd, ALU.mult), reads=[psr[pb], rstd_r], writes=[y_r])
                op("act", lambda e: e.activation(sqy, y, AF.Square), reads=[y_r], writes=[sqy_r])

            def s1():
                mm(PS(ab), ones, sqy, True, True, [ones_r, sqy_r], [psr[ab]])
                rstd_ps(PS(ab), psr[ab], 128.0, tmp, tmp_r, rq, rq_r)
                op("dve", lambda e: e.scalar_tensor_tensor(z, y, C(gname), rq, ALU.mult, ALU.mult),
                   reads=[y_r, rq_r] + CR, writes=[z_r])

            def s2():
                self.rope_stage(z, z_r, perm, perm_r, cosT, sinT, tab_rs, 128, dst)
            return [s0, s1, s2]

        def item_rope(self, W, W_r, col0, xg, xg_r, rstd, rstd_r, perm, perm_r, cosT, sinT, tab_rs, dst):
            pb = self.pb.next()
            z, z_r = self.zs.next()

            def s0():
                self.proj(W, W_r, col0, 128, xg, xg_r, pb)
                op("dve", lambda e: e.tensor_tensor(z, PS(pb), rstd, ALU.mult), reads=[psr[pb], rstd_r], writes=[z_r])

            def s1():
                self.rope_stage(z, z_r, perm, perm_r, cosT, sinT, tab_rs, 128, dst)
            return [s0, s1]

    def phase1():
        A.mark()
        NK = c.HKV * 128 * 2 + 128 + c.KVL + 64
        o_ka, o_va, o_ki, o_ckv, o_kr = 0, c.HKV * 128, c.HKV * 256, c.HKV * 256 + 128, c.HKV * 256 + 128 + c.KVL
        Wk, Wk_r = A.alloc("Wk", [DC, NK], BF16)
        wsrc = wb["w_in"][0].rearrange("(c p) n -> p c n", p=P)
        wr = wb["w_in"][2]
        for (dst0, src0, n) in ((o_ka, c.o_ka, c.HKV * 256), (o_ki, c.o_ki, 64), (o_ki + 64, c.o_ki, 64),
                                (o_ckv, c.o_ckv, c.KVL + 64)):
            dma("sp", lambda e: e.dma_start(out=Wk[:, :, dst0:dst0 + n], in_=wsrc[:, :, src0:src0 + n]),
                reads=[wr], writes=[Wk_r])
        X, X_r = A.alloc("X", [DC, TG], F32)
        xgs = [A.alloc(f"xg{i}", [DC, TG], BF16) for i in range(2)]
        rstds = [A.alloc(f"rstd{i}", [TG], F32) for i in range(2)]
        cosA, cosA_r = A.alloc("cosA", [TG], F32)
        sinA, sinA_r = A.alloc("sinA", [TG], F32)
        cosI, cosI_r = A.alloc("cosI", [TG], F32)
        sinI, sinI_r = A.alloc("sinI", [TG], F32)
        ycs = [A.alloc(f"yc{i}", [c.KC, TG], F32) for i in range(2)]
        sqc, sqc_r = A.alloc("sqc", [c.KC, TG], BF16)
        vts = [A.alloc(f"vt{i}", [4, 130], BF16) for i in range(3)]
        for vt, vt_r in vts:
            op("dve", lambda e: e.memset(vt, 0.0), writes=[vt_r])
            op("dve", lambda e: e.memset(vt[:, :, 128:129], 1.0), writes=[vt_r])
        vtr = Rot(vts)
        K_ = ProjCtx()
        tabA = [cosA_r, sinA_r]
        tabI = [cosI_r, sinI_r]
        items = []
        for g in range(NKG):
            t0 = g * TG
            xg, xg_r = xgs[g % 2]
            rstd, rstd_r = rstds[g % 2]
            yc, yc_r = ycs[g % 2]

            def prep_item(t0=t0, xg=xg, xg_r=xg_r, rstd=rstd, rstd_r=rstd_r):
                load_x_group(xT_all, t0, X, X_r)
                K_.prep(X, X_r, "g_mix", xg, xg_r, rstd, rstd_r)

            def tab_item(t0=t0):
                for (tile, tr, nm) in ((cosA, cosA_r, "cosA_k"), (sinA, sinA_r, "sinA_k"), (cosI, cosI_r, "cosI_k"),
                                       (sinI, sinI_r, "sinI_k")):
                    dma("sp", lambda e: e.dma_start(out=tile, in_=tabs[nm][:, t0:t0 + TG]), writes=[tr])
            if g == 0:
                items.append([prep_item])
                items.append([tab_item])
            grp = []
            for h in range(c.HKV):
                grp.append(K_.item_normrope(Wk, Wk_r, o_ka + h * 128, "g_ka", xg, xg_r, rstd, rstd_r, permA, permA_r,
                                            cosA, sinA, tabA, kaT[h, :, t0:t0 + TG]))
            grp.append(K_.item_rope(Wk, Wk_r, o_ki, xg, xg_r, rstd, rstd_r, permI, permI_r, cosI, sinI, tabI,
                                    kiT2[:, t0:t0 + TG]))
            def mk_kr(t0=t0, xg=xg, xg_r=xg_r, rstd=rstd, rstd_r=rstd_r):
                pb = K_.pb.next()
                y, y_r = K_.ys.next()
                zr, zr_r = K_.zrs.next()
                ob, ob_r = K_.obs.next()

                def s0():
                    K_.proj(Wk, Wk_r, o_kr, 64, xg, xg_r, pb)
                    op("dve", lambda e: e.tensor_tensor(y[0:64, :], PS(pb)[0:64, :], rstd[0:64, :], ALU.mult),
                       reads=[psr[pb], rstd_r], writes=[y_r])
                    op("act", lambda e: e.activation(ob[0:64, :], y[0:64, :], AF.Square), reads=[y_r], writes=[ob_r])
                    dma("sp", lambda e: e.dma_start(out=kr2T[:, t0:t0 + TG], in_=ob[0:64, :]), reads=[ob_r])
                    op("dve", lambda e: e.tensor_scalar(zr[0:64, :], y[0:64, :], C("g_kbr", 0, 64), None, ALU.mult),
                       reads=[y_r] + CR, writes=[zr_r])

                def s1():
                    K_.rope_stage(zr, zr_r, permI, permI_r, cosI, sinI, tabI, 64, krgrT[:, t0:t0 + TG])
                return [s0, s1]
            grp.append(mk_kr())
            for h in range(c.HKV):
                def mk_va(h=h, t0=t0, g=g, xg=xg, xg_r=xg_r, rstd=rstd, rstd_r=rstd_r):
                    pb, ab = K_.pb.next(), K_.ab.next()
                    z, z_r = K_.zs.next()
                    vt, vt_r = vtr.next()

                    def s0():
                        K_.proj(Wk, Wk_r, o_va + h * 128, 128, xg, xg_r, pb)
                        op("dve", lambda e: e.tensor_tensor(z, PS(pb), rstd, ALU.mult), reads=[psr[pb], rstd_r], writes=[z_r])

                    def s1():
                        for tb in range(4):
                            mm(PS(ab)[:, tb * 128:(tb + 1) * 128], z[:, tb * 128:(tb + 1) * 128], ident, True, True,
                               [z_r, ident_r], [psr[ab]])
                        op("act", lambda e: e.activation(vt[:, :, 0:128], PS(ab).rearrange("p (a b) -> p a b", b=128), AF.Copy),
                           reads=[psr[ab]], writes=[vt_r])
                        dma("sp", lambda e: e.dma_start(out=va_tok[h, :, 4 * g:4 * g + 4, :], in_=vt), reads=[vt_r])
                    return [s0, s1]
                grp.append(mk_va())
            for kc in range(c.KC):
                def mk_ckv(kc=kc, t0=t0, xg=xg, xg_r=xg_r, rstd=rstd, rstd_r=rstd_r, yc=yc, yc_r=yc_r):
                    pb = K_.pb.next()
                    last = kc == c.KC - 1
                    if last:
                        ab = K_.ab.next()
                        rq, rq_r = K_.rqs.next()
                        tmp, tmp_r = K_.tmps.next()

                    def s0():
                        K_.proj(Wk, Wk_r, o_ckv + kc * 128, 128, xg, xg_r, pb)
                        op("dve", lambda e: e.tensor_tensor(yc[:, kc, :], PS(pb), rstd, ALU.mult),
                           reads=[psr[pb], rstd_r], writes=[yc_r])
                        if last:
                            op("act", lambda e: e.activation(sqc, yc, AF.Square), reads=[yc_r], writes=[sqc_r])

                    def s1():
                        if not last:
                            return
                        for k2 in range(c.KC):
                            mm(PS(ab), ones, sqc[:, k2, :], k2 == 0, k2 == c.KC - 1, [ones_r, sqc_r], [psr[ab]])
                        rstd_ps(PS(ab), psr[ab], float(c.KVL), tmp, tmp_r, rq, rq_r)
                        for k2 in range(c.KC):
                            ob, ob_r = K_.obs.next()
                            op("dve", lambda e: e.scalar_tensor_tensor(ob, yc[:, k2, :], C("g_ckv", k2), rq, ALU.mult, ALU.mult),
                               reads=[yc_r, rq_r] + CR, writes=[ob_r])
                            dma("sp", lambda e: e.dma_start(out=ckvnT[k2, :, t0:t0 + TG], in_=ob), reads=[ob_r])
                    return [s0, s1]
                grp.append(mk_ckv())
            if g + 1 < NKG:
                xg2, xg2_r = xgs[(g + 1) % 2]
                rstd2, rstd2_r = rstds[(g + 1) % 2]

                def load_next(t1_=(g + 1) * TG):
                    load_x_group(xT_all, t1_, X, X_r)

                def prep_next(xg2=xg2, xg2_r=xg2_r, rstd2=rstd2, rstd2_r=rstd2_r):
                    K_.prep(X, X_r, "g_mix", xg2, xg2_r, rstd2, rstd2_r)

                def tab_next(t1_=(g + 1) * TG):
                    for (tile, tr, nm) in ((cosA, cosA_r, "cosA_k"), (sinA, sinA_r, "sinA_k"), (cosI, cosI_r, "cosI_k"),
                                           (sinI, sinI_r, "sinI_k")):
                        dma("sp", lambda e: e.dma_start(out=tile, in_=tabs[nm][:, t1_:t1_ + TG]), writes=[tr])
                n_rope = c.HKV + 2
                grp.insert(1, [load_next])
                grp.insert(min(len(grp), n_rope + 3), [tab_next])
                grp.append([prep_next])
            items.extend(grp)
        run_pipeline(items)
        A.release()
        S_.barrier()

    def phase2():
        A.mark()
        X, X_r = A.alloc("X", [DC, TG], F32)
        xgs = [A.alloc(f"xg{i}", [DC, TG], BF16) for i in range(1)]
        rstds = [A.alloc(f"rstd{i}", [TG], F32) for i in range(2)]
        Wuq, Wuq_r = A.alloc("Wuq", [c.QC, c.HB * 192], BF16)
        dma("sp", lambda e: e.dma_start(out=Wuq, in_=wb["w_uq"][0].rearrange("(c p) n -> p c n", p=P)),
            reads=[wb["w_uq"][2]], writes=[Wuq_r])
        Ws = Rot([A.alloc(f"W{i}", [DC, 512], BF16) for i in range(2)])
        cA, cA_r = A.alloc("cA", [TG], F32)
        sA, sA_r = A.alloc("sA", [TG], F32)
        cI, cI_r = A.alloc("cI", [TG], F32)
        sI, sI_r = A.alloc("sI", [TG], F32)
        ycq, ycq_r = A.alloc("ycq", [c.QC, TG], F32)
        sqc, sqc_r = A.alloc("sqc", [c.QC, TG], BF16)
        cqn, cqn_r = A.alloc("cqn", [c.QC, TG], BF16)
        sqrs = [A.alloc(f"sqr{i}", [TG], BF16) for i in range(2)]
        for t, r in sqrs:
            op("dve", lambda e: e.memset(t, 0.0), writes=[r])
        sqrr = Rot(sqrs)
        wit, wit_r = A.alloc("wit", [TG], F32)
        wio, wio_r = A.alloc("wio", [4, c.HI], F32)
        K_ = ProjCtx()
        wsrc = wb["w_in"][0].rearrange("(c p) n -> p c n", p=P)
        wr = wb["w_in"][2]
        ranges = [("qa", c.o_qa, c.HA * 128), ("qi", c.o_qi, c.HI * 64), ("wi", c.o_wi, c.HI),
                  ("cq", c.o_cq, c.QL), ("g", c.o_g, 2 * D)]
        loads = []
        for kind, c0, n in ranges:
            for l0 in range(0, n, 512):
                loads.append((kind, c0, l0, min(512, n - l0)))
        HI = c.HI
        tabA = [cA_r, sA_r]
        tabI = [cI_r, sI_r]
        items = []
        allloads = [(g, L) for g in range(NQG) for L in loads]
        wbufs = [Ws.next() for _ in allloads]

        def mk_load(j):
            (g_, (kind_, cbase_, l0_, ln_)) = allloads[j]
            W_, W_r_ = wbufs[j]

            def load_item():
                dma("sp", lambda e: e.dma_start(out=W_[:, :, 0:ln_], in_=wsrc[:, :, cbase_ + l0_:cbase_ + l0_ + ln_]),
                    reads=[wr], writes=[W_r_])
            return [load_item]
        items.append(mk_load(0))
        lj = 0
        for g in range(NQG):
            t0 = g * TG
            xg, xg_r = xgs[0]
            rstd, rstd_r = rstds[g % 2]

            def prep_item(t0=t0, xg=xg, xg_r=xg_r, rstd=rstd, rstd_r=rstd_r):
                load_x_group(xT_own, t0, X, X_r)
                for (tile, tr, nm) in ((cA, cA_r, "cosA_q"), (sA, sA_r, "sinA_q"), (cI, cI_r, "cosI_q"), (sI, sI_r, "sinI_q")):
                    dma("sp", lambda e: e.dma_start(out=tile, in_=tabs[nm][:, t0:t0 + TG]), writes=[tr])
                K_.prep(X, X_r, "g_mix", xg, xg_r, rstd, rstd_r)
            items.append([prep_item])
            for (kind, cbase, l0, ln) in loads:
                W, W_r = wbufs[lj]
                lj += 1
                if lj < len(allloads):
                    items.append(mk_load(lj))
                for m0 in range(0, ln, 128):
                    M = min(128, ln - m0)
                    idx = (l0 + m0) // 128
                    if kind == "qa":
                        items.append(K_.item_normrope(W, W_r, m0, "g_qa", xg, xg_r, rstd, rstd_r, permA, permA_r,
                                                      cA, sA, tabA, qaT[idx, :, t0:t0 + TG]))
                    elif kind == "qi":
                        items.append(K_.item_rope(W, W_r, m0, xg, xg_r, rstd, rstd_r, permI, permI_r, cI, sI, tabI,
                                                  qiT[idx, :, t0:t0 + TG]))
                    elif kind == "wi":
                        def mk_wi(W=W, W_r=W_r, m0=m0, t0=t0, xg=xg, xg_r=xg_r, rstd=rstd, rstd_r=rstd_r):
                            pb, ab = K_.pb.next(), K_.ab.next()

                            def s0():
                                K_.proj(W, W_r, m0, HI, xg, xg_r, pb)
                                op("dve", lambda e: e.scalar_tensor_tensor(
                                    wit[0:HI, :], PS(pb)[0:HI, :], (c.HI ** -0.5) * (c.DI ** -0.5), rstd[0:HI, :],
                                    ALU.mult, ALU.mult), reads=[psr[pb], rstd_r], writes=[wit_r])

                            def s1():
                                for tb in range(4):
                                    mm(PS(ab)[:, tb * HI:(tb + 1) * HI], wit[0:HI, tb * 128:(tb + 1) * 128], identf[0:HI, 0:HI],
                                       True, True, [wit_r, identf_r], [psr[ab]])
                                op("dve", lambda e: e.tensor_copy(wio, PS(ab)[:, 0:4 * HI].rearrange("p (a b) -> p a b", b=HI)),
                                   reads=[psr[ab]], writes=[wio_r])
                                dma("sp", lambda e: e.dma_start(
                                    out=wi_tok[t0:t0 + TG, :].rearrange("(a p) h -> p a h", p=P), in_=wio), reads=[wio_r])
                            return [s0, s1]
                        items.append(mk_wi())
                    elif kind == "cq":
                        def mk_cq(W=W, W_r=W_r, m0=m0, idx=idx, xg=xg, xg_r=xg_r, rstd=rstd, rstd_r=rstd_r):
                            pb = K_.pb.next()
                            last = idx == c.QC - 1
                            if last:
                                ab = K_.ab.next()
                                rq, rq_r = K_.rqs.next()
                                tmp, tmp_r = K_.tmps.next()

                            def s0():
                                K_.proj(W, W_r, m0, 128, xg, xg_r, pb)
                                op("dve", lambda e: e.tensor_tensor(ycq[:, idx, :], PS(pb), rstd, ALU.mult),
                                   reads=[psr[pb], rstd_r], writes=[ycq_r])
                                if last:
                                    op("act", lambda e: e.activation(sqc, ycq, AF.Square), reads=[ycq_r], writes=[sqc_r])

                            def s1():
                                if not last:
                                    return
                                for qc in range(c.QC):
                                    mm(PS(ab), ones, sqc[:, qc, :], qc == 0, qc == c.QC - 1, [ones_r, sqc_r], [psr[ab]])
                                rstd_ps(PS(ab), psr[ab], float(c.QL), tmp, tmp_r, rq, rq_r)
                                for qc in range(c.QC):
                                    op("dve", lambda e: e.scalar_tensor_tensor(cqn[:, qc, :], ycq[:, qc, :], C("g_cq", qc), rq,
                                                                               ALU.mult, ALU.mult),
                                       reads=[ycq_r, rq_r] + CR, writes=[cqn_r])
                            return [s0, s1]
                        items.append(mk_cq())
                        if idx == c.QC - 1:
                            items.append([])
                            items.append([])
                            for h in range(c.HB):
                                def mk_qbh(h=h, t0=t0):
                                    p1, p2, ab = K_.pb.next(), K_.pb.next(), K_.ab.next()
                                    sqy, sqy_r = K_.sqs.next()
                                    sqr, sqr_r = sqrr.next()
                                    rq, rq_r = K_.rqs.next()
                                    tmp, tmp_r = K_.tmps.next()
                                    zr, zr_r = K_.zrs.next()
                                    ob, ob_r = K_.obs.next()

                                    def s0():
                                        for qc in range(c.QC):
                                            mm(PS(p1), Wuq[:, qc, h * 192:h * 192 + 128], cqn[:, qc, :], qc == 0, qc == c.QC - 1,
                                               [Wuq_r, cqn_r], [psr[p1]])
                                        for qc in range(c.QC):
                                            mm(PS(p2)[0:64, :], Wuq[:, qc, h * 192 + 128:h * 192 + 192], cqn[:, qc, :], qc == 0,
                                               qc == c.QC - 1, [Wuq_r, cqn_r], [psr[p2]])
                                        op("act", lambda e: e.activation(sqy, PS(p1), AF.Square), reads=[psr[p1]], writes=[sqy_r])
                                        op("act", lambda e: e.activation(sqr[0:64, :], PS(p2)[0:64, :], AF.Square),
                                           reads=[psr[p2]], writes=[sqr_r])

                                    def s1():
                                        mm(PS(ab), ones, sqy, True, False, [ones_r, sqy_r], [psr[ab]])
                                        mm(PS(ab), ones, sqr, False, True, [ones_r, sqr_r], [psr[ab]])
                                        rstd_ps(PS(ab), psr[ab], 192.0, tmp, tmp_r, rq, rq_r)
                                        op("dve", lambda e: e.scalar_tensor_tensor(ob, PS(p1), C("g_qbn"), rq, ALU.mult, ALU.mult),
                                           reads=[psr[p1], rq_r] + CR, writes=[ob_r])
                                        dma("sp", lambda e: e.dma_start(out=qbnT[h, :, t0:t0 + TG], in_=ob), reads=[ob_r])
                                        op("dve", lambda e: e.scalar_tensor_tensor(zr[0:64, :], PS(p2)[0:64, :], C("g_qbr", 0, 64),
                                                                                   rq[0:64, :], ALU.mult, ALU.mult),
                                           reads=[psr[p2], rq_r] + CR, writes=[zr_r])

                                    def s2():
                                        K_.rope_stage(zr, zr_r, permI, permI_r, cI, sI, tabI, 64, qbrT[h, :, t0:t0 + TG])
                                    return [s0, s1, s2]
                                items.append(mk_qbh())
                    else:
                        def mk_gate(W=W, W_r=W_r, m0=m0, idx=idx, t0=t0, xg=xg, xg_r=xg_r, rstd=rstd, rstd_r=rstd_r):
                            pb = K_.pb.next()
                            y, y_r = K_.ys.next()
                            ob, ob_r = K_.obs.next()

                            def s0():
                                K_.proj(W, W_r, m0, 128, xg, xg_r, pb)
                                op("dve", lambda e: e.tensor_tensor(y, PS(pb), rstd, ALU.mult), reads=[psr[pb], rstd_r], writes=[y_r])

                            def s1():
                                op("act", lambda e: e.activation(ob, y, AF.Sigmoid), reads=[y_r], writes=[ob_r])
                                dma("sp", lambda e: e.dma_start(out=gatesT[idx, :, t0:t0 + TG], in_=ob), reads=[ob_r])
                            return [s0, s1]
                        items.append(mk_gate())
        run_pipeline(items)
        A.release()
        S_.barrier()

    def phase3():
        A.mark()
        ki, ki_r = A.alloc("kiT2", [S], BF16)
        dma("sp", lambda e: e.dma_start(out=ki, in_=kiT2), writes=[ki_r])
        posq_i, posq_ir = A.alloc("posq_i", [NQB], I32)
        posq, posq_r = A.alloc("posq", [NQB], F32)
        dma("sp", lambda e: e.dma_start(out=posq_i, in_=pos_ownT), writes=[posq_ir])
        op("dve", lambda e: e.tensor_copy(posq, posq_i), reads=[posq_ir], writes=[posq_r])
        scs = [A.alloc(f"scores{i}", [S], F32) for i in range(2)]
        nDmax = max(P, int(round(0.42 * NB)) * P)
        junk, junk_r = A.alloc("junk", [nDmax], BF16)
        junkA, junkA_r = A.alloc("junkA", [S - P], BF16)
        accA, accA_r = A.alloc("accA", [1], F32)
        mask, mask_r = A.alloc("mask", [S], BF16)
        mts = [A.alloc(f"mT{i}", [NB, P], BF16) for i in range(2)]
        qis = [A.alloc(f"qi{i}", [c.HI // 2, P], BF16) for i in range(2)]
        wis = [A.alloc(f"wi{i}", [c.HI], F32) for i in range(2)]
        rls = [A.alloc(f"rl{i}", [TG], F32) for i in range(3)]
        pkb, pkb_r = A.alloc("pkb", [TG], F32)
        pen, pen_r = A.alloc("pen", [TG], F32)
        amax, amax_r = A.alloc("amax", [1], F32)
        lo, lo_r = A.alloc("lo", [1], F32)
        W0, W0_r = A.alloc("W0", [1], F32)
        mid, mid_r = A.alloc("mid", [1], F32)
        cnt, cnt_r = A.alloc("cnt", [1], F32)
        mw, mw_r = A.alloc("mw", [1], F32)
        rl_i = [0]

        def indexer_steps(qb):
            nsb = 4 * qb + 4
            ns = nsb * P
            qi_, qi_r = qis[qb % 2]
            wi_, wi_r = wis[qb % 2]
            scores, sc_r = scs[qb % 2]
            out = []

            def loads():
                dma("sp", lambda e: e.dma_start(out=qi_, in_=qiT[:, :, qb * P:(qb + 1) * P].rearrange("c p t -> p c t")),
                    writes=[qi_r])
                dma("sp", lambda e: e.dma_start(out=wi_, in_=wi_tok[qb * P:(qb + 1) * P, :]), writes=[wi_r])
            out.append(loads)
            for sc in range(ns // TG):
                for h in range(c.HI):
                    def step(sc=sc, h=h):
                        psi = h % 2
                        hb = (h % 2) * 64
                        mm(PS(psi), qi_[hb:hb + 64, h // 2, :], ki[hb:hb + 64, sc * TG:(sc + 1) * TG], True, True,
                           [qi_r, ki_r], [psr[psi]])
                        rl, rl_r = rls[rl_i[0] % 3]
                        rl_i[0] += 1
                        op("act", lambda e: e.activation(rl, PS(psi), AF.Relu), reads=[psr[psi]], writes=[rl_r])
                        dst = scores[:, sc * TG:(sc + 1) * TG]
                        if h == 0:
                            op("dve", lambda e: e.tensor_scalar(dst, rl, wi_[:, h:h + 1], None, ALU.mult),
                               reads=[rl_r, wi_r], writes=[sc_r])
                        else:
                            op("dve", lambda e: e.scalar_tensor_tensor(dst, rl, wi_[:, h:h + 1], dst, ALU.mult, ALU.add),
                               reads=[rl_r, wi_r, sc_r], writes=[sc_r])
                    out.append(step)
            return out
        for f in indexer_steps(0):
            f()
        for qb in range(NQB):
            nsb = 4 * qb + 4
            ns = nsb * P
            scores, sc_r = scs[qb % 2]
            nxt = indexer_steps(qb + 1) if qb + 1 < NQB else []
            per = (len(nxt) + c.NBIS - 1) // c.NBIS
            dma("sp", lambda e: e.dma_start(out=pkb, in_=posf_all[0:1, 4 * qb * P:(4 * qb + 4) * P].partition_broadcast(P)),
                writes=[pkb_r])
            sv = scores[:, 0:ns]
            op("dve", lambda e: e.tensor_reduce(amax, sv, AX.X, ALU.max, apply_absolute_value=True),
               reads=[sc_r], writes=[amax_r])
            dg = scores[:, ns - TG:ns]
            op("dve", lambda e: e.tensor_scalar(pen, pkb, posq[:, qb:qb + 1], 1.0, ALU.is_le, ALU.subtract),
               reads=[pkb_r, posq_r], writes=[pen_r])
            op("dve", lambda e: e.scalar_tensor_tensor(dg, pen, NEG_BIG, dg, ALU.mult, ALU.add),
               reads=[pen_r, sc_r], writes=[sc_r])
            op("dve", lambda e: e.tensor_scalar(lo, amax, -1.0, -1.0, ALU.mult, ALU.add), reads=[amax_r], writes=[lo_r])
            op("dve", lambda e: e.tensor_scalar(W0, amax, 2.0, 2.0, ALU.mult, ALU.add), reads=[amax_r], writes=[W0_r])
            nD = max(P, int(round(0.40 * nsb)) * P)
            nA = ns - nD
            for it in range(c.NBIS):
                hw = 0.5 ** (it + 1)
                op("dve", lambda e: e.scalar_tensor_tensor(mid, W0, hw, lo, ALU.mult, ALU.add),
                   reads=[W0_r, lo_r], writes=[mid_r])
                op("act", lambda e: e.activation(junkA[:, 0:nA], scores[:, nD:ns], AF.Sign, bias=mid, scale=-1.0, accum_out=accA),
                   reads=[sc_r, mid_r], writes=[junkA_r, accA_r])
                op("dve", lambda e: e.tensor_scalar(junk[:, 0:nD], scores[:, 0:nD], mid, 0.0, ALU.is_ge, ALU.add, accum_out=cnt),
                   reads=[sc_r, mid_r], writes=[junk_r, cnt_r])
                for f in nxt[it * per:(it + 1) * per]:
                    f()
                op("dve", lambda e: e.scalar_tensor_tensor(cnt, accA, -0.5, cnt, ALU.mult, ALU.add),
                   reads=[accA_r, cnt_r], writes=[cnt_r])
                op("dve", lambda e: e.tensor_scalar(mw, cnt, float(c.KSEL) - 0.5 - 0.5 * nA, W0, ALU.is_ge, ALU.mult),
                   reads=[cnt_r, W0_r], writes=[mw_r])
                op("dve", lambda e: e.scalar_tensor_tensor(lo, mw, hw, lo, ALU.mult, ALU.add),
                   reads=[mw_r, lo_r], writes=[lo_r])
            for f in nxt[c.NBIS * per:]:
                f()
            op("dve", lambda e: e.tensor_scalar(mask[:, 0:ns], sv, lo, None, ALU.is_ge),
               reads=[sc_r, lo_r], writes=[mask_r])
            mT, mT_r = mts[qb % 2]
            for s4 in range(nsb // 4):
                psi = 2 + (s4 % 2)
                for k in range(4):
                    sb = s4 * 4 + k
                    mm(PS(psi)[:, k * P:(k + 1) * P], mask[:, sb * P:(sb + 1) * P], ident, True, True,
                       [mask_r, ident_r], [psr[psi]])
                op("act", lambda e: e.activation(
                    mT[:, 4 * s4:4 * s4 + 4, :], PS(psi).rearrange("p (a b) -> p a b", b=P), AF.Copy),
                   reads=[psr[psi]], writes=[mT_r])
            dma("sp", lambda e: e.dma_start(
                out=maskT[qb, :, 0:ns].rearrange("p (a b) -> p a b", b=P), in_=mT[:, 0:nsb, :]), reads=[mT_r])
        A.release()
        S_.barrier()

    def phase4():
        A.mark()
        scaleA = 128.0 ** -0.5
        LA = 2
        ka, ka_r = A.alloc("ka", [S], BF16)
        va, va_r = A.alloc("va", [NB, 130], BF16)
        mts = [A.alloc(f"mT{i}", [NB, P], BF16) for i in range(2)]
        qas = [A.alloc(f"qa{i}", [4, P], BF16) for i in range(2)]
        NE = 4
        es_ = [A.alloc(f"e{i}", [4, P], BF16) for i in range(NE)]
        pms = [A.alloc(f"pm{i}", [4, P], BF16) for i in range(NE)]
        rden, rden_r = A.alloc("rden", [4], F32)
        yt, yt_r = A.alloc("yt", [4, P], BF16)
        yos = [A.alloc(f"yo{i}", [4, P], BF16) for i in range(2)]
        sbank = (0, 1, 7)
        steps = []
        for kvh in range(c.HKV):
            for qb in range(NQB):
                for sb in range(4 * qb + 4):
                    steps.append((kvh, qb, sb))
        ctx = {}
        it = [0]

        def prologue(kvh, qb):
            nsb = 4 * qb + 4
            mT, mT_r = mts[it[0] % 2]
            qa_, qa_r = qas[it[0] % 2]
            yo, yo_r = yos[it[0] % 2]
            it[0] += 1
            if qb == 0:
                dma("sp", lambda e: e.dma_start(out=ka, in_=kaT[kvh]), writes=[ka_r])
                dma("sp", lambda e: e.dma_start(out=va, in_=va_tok[kvh]), writes=[va_r])
            dma("sp", lambda e: e.dma_start(
                out=mT[:, 0:nsb, :], in_=maskT[qb, :, 0:nsb * P].rearrange("p (a b) -> p a b", b=P)), writes=[mT_r])
            dma("sp", lambda e: e.dma_start(
                out=qa_, in_=qaT[4 * kvh:4 * kvh + 4, :, qb * P:(qb + 1) * P].rearrange("g p t -> p g t")), writes=[qa_r])
            ctx[(kvh, qb)] = (mT, mT_r, qa_, qa_r, yo, yo_r)

        def stage1(i):
            kvh, qb, sb = steps[i]
            if sb == 0:
                prologue(kvh, qb)
            mT, mT_r, qa_, qa_r, yo, yo_r = ctx[(kvh, qb)]
            psi = sbank[i % 3]
            e_, e_r = es_[i % NE]
            pm, pm_r = pms[i % NE]
            mm(PS(psi), ka[:, sb * P:(sb + 1) * P], qa_.rearrange("p g t -> p (g t)"), True, True,
               [ka_r, qa_r], [psr[psi]])
            op("act", lambda e: e.activation(e_.rearrange("p g t -> p (g t)"), PS(psi), AF.Exp, scale=scaleA),
               reads=[psr[psi]], writes=[e_r])
            op("dve", lambda e: e.tensor_tensor(pm, e_, mT[:, sb:sb + 1, :].broadcast_to([P, 4, P]), ALU.mult),
               reads=[e_r, mT_r], writes=[pm_r])

        def stage2(i):
            kvh, qb, sb = steps[i]
            nsb = 4 * qb + 4
            mT, mT_r, qa_, qa_r, yo, yo_r = ctx[(kvh, qb)]
            pm, pm_r = pms[i % NE]
            for g in range(4):
                mm(PS(2 + g)[:, 0:130], pm[:, g, :], va[:, sb, :], sb == 0, sb == nsb - 1,
                   [pm_r, va_r], [psr[2 + g]])
            if sb == nsb - 1:
                for g in range(4):
                    op("dve", lambda e, g=g: e.reciprocal(rden[:, g:g + 1], PS(2 + g)[:, 128:129]),
                       reads=[psr[2 + g]], writes=[rden_r])
                    op("dve", lambda e, g=g: e.tensor_scalar(yt[:, g, :], PS(2 + g)[:, 0:128], rden[:, g:g + 1], None, ALU.mult),
                       reads=[psr[2 + g], rden_r], writes=[yt_r])
                for g in range(4):
                    mm(PS(6)[:, g * P:(g + 1) * P], yt[:, g, :], ident, True, True, [yt_r, ident_r], [psr[6]])
                op("act", lambda e: e.activation(yo.rearrange("p g t -> p (g t)"), PS(6), AF.Copy),
                   reads=[psr[6]], writes=[yo_r])
                dma("sp", lambda e: e.dma_start(
                    out=yaT[4 * kvh:4 * kvh + 4, :, qb * P:(qb + 1) * P].rearrange("g p t -> p g t"), in_=yo), reads=[yo_r])
        n = len(steps)
        for i in range(n + LA):
            if i < n:
                stage1(i)
            if i >= LA:
                stage2(i - LA)
        A.release()
        S_.barrier()

    def phase5a():
        A.mark()
        KC = c.KC
        ckv, ckv_r = A.alloc("ckv", [KC, S], BF16)
        krg, krg_r = A.alloc("krg", [S], BF16)
        kr2, kr2_r = A.alloc("kr2", [S], BF16)
        Wkv, Wkv_r = A.alloc("Wkv", [KC, c.HB * 256], BF16)
        dma("sp", lambda e: e.dma_start(out=ckv, in_=ckvnT.rearrange("c p s -> p c s")), writes=[ckv_r])
        dma("sp", lambda e: e.dma_start(out=krg[0:64, :], in_=krgrT), writes=[krg_r])
        op("dve", lambda e: e.memset(kr2, 0.0), writes=[kr2_r])
        dma("sp", lambda e: e.dma_start(out=kr2[0:64, :], in_=kr2T), writes=[kr2_r])
        dma("sp", lambda e: e.dma_start(out=Wkv, in_=wb["w_ukv"][0].rearrange("(c p) n -> p c n", p=P)),
            reads=[wb["w_ukv"][2]], writes=[Wkv_r])
        NBF = 3
        sqys = [A.alloc(f"sqy{i}", [TG], BF16) for i in range(NBF)]
        vTs = [A.alloc(f"vT{i}", [TG], BF16) for i in range(NBF)]
        tmps = [A.alloc(f"tmp{i}", [TG], F32) for i in range(NBF)]
        rqs = [A.alloc(f"rq{i}", [TG], F32) for i in range(NBF)]
        knts = [A.alloc(f"knt{i}", [TG], BF16) for i in range(NBF)]
        krts = [A.alloc(f"krt{i}", [TG], BF16) for i in range(NBF)]
        vts = [A.alloc(f"vt{i}", [4, 130], BF16) for i in range(NBF)]
        for vt, vt_r in vts:
            op("dve", lambda e: e.memset(vt, 0.0), writes=[vt_r])
            op("dve", lambda e: e.memset(vt[:, :, 128:129], 1.0), writes=[vt_r])
        bA, bB, bC, bD = (0, 1, 2), (3, 4), (5, 6), (7,)
        items = [(h, g) for h in range(c.HB) for g in range(NKG)]

        def st0(i):
            h, g = items[i]
            t0 = g * TG
            pa, pb = bA[i % 3], bB[i % 2]
            sqy, sqy_r = sqys[i % NBF]
            vT, vT_r = vTs[i % NBF]
            for kc in range(KC):
                mm(PS(pa), Wkv[:, kc, h * 256:h * 256 + 128], ckv[:, kc, t0:t0 + TG], kc == 0, kc == KC - 1,
                   [Wkv_r, ckv_r], [psr[pa]])
            for kc in range(KC):
                mm(PS(pb), Wkv[:, kc, h * 256 + 128:h * 256 + 256], ckv[:, kc, t0:t0 + TG], kc == 0, kc == KC - 1,
                   [Wkv_r, ckv_r], [psr[pb]])
            op("act", lambda e: e.activation(sqy, PS(pa), AF.Square), reads=[psr[pa]], writes=[sqy_r])
            op("dve", lambda e: e.tensor_copy(vT, PS(pb)), reads=[psr[pb]], writes=[vT_r])

        def st1(i):
            h, g = items[i]
            t0 = g * TG
            pc, pd = bC[i % 2], bD[0]
            sqy, sqy_r = sqys[i % NBF]
            vT, vT_r = vTs[i % NBF]
            tmp, tmp_r = tmps[i % NBF]
            rq, rq_r = rqs[i % NBF]
            vt, vt_r = vts[i % NBF]
            mm(PS(pc), ones, sqy, True, False, [ones_r, sqy_r], [psr[pc]])
            mm(PS(pc), ones, kr2[:, t0:t0 + TG], False, True, [ones_r, kr2_r], [psr[pc]])
            for tb in range(4):
                mm(PS(pd)[:, tb * P:(tb + 1) * P], vT[:, tb * P:(tb + 1) * P], ident, True, True, [vT_r, ident_r], [psr[pd]])
            rstd_ps(PS(pc), psr[pc], 192.0, tmp, tmp_r, rq, rq_r)
            op("dve", lambda e: e.tensor_copy(vt[:, :, 0:128], PS(pd).rearrange("p (a b) -> p a b", b=P)),
               reads=[psr[pd]], writes=[vt_r])
            dma("sp", lambda e: e.dma_start(out=V_all[h, :, 4 * g:4 * g + 4, :], in_=vt), reads=[vt_r])

        def st2(i):
            h, g = items[i]
            t0 = g * TG
            pa = bA[i % 3]
            rq, rq_r = rqs[i % NBF]
            knt, knt_r = knts[i % NBF]
            krt, krt_r = krts[i % NBF]
            op("dve", lambda e: e.scalar_tensor_tensor(knt, PS(pa), C("g_kbn"), rq, ALU.mult, ALU.mult),
               reads=[psr[pa], rq_r] + CR, writes=[knt_r])
            op("dve", lambda e: e.tensor_tensor(krt[0:64, :], krg[0:64, t0:t0 + TG], rq[0:64, :], ALU.mult),
               reads=[krg_r, rq_r], writes=[krt_r])
            dma("sp", lambda e: e.dma_start(out=KnT_all[h, :, t0:t0 + TG], in_=knt), reads=[knt_r])
            dma("sp", lambda e: e.dma_start(out=KrT_all[h, :, t0:t0 + TG], in_=krt[0:64, :]), reads=[krt_r])
        n = len(items)
        for i in range(n + 2):
            if i < n:
                st0(i)
            if 1 <= i < n + 1:
                st1(i - 1)
            if i >= 2:
                st2(i - 2)
        A.release()
        S_.barrier()

    def phase5b():
        A.mark()
        scaleB = 192.0 ** -0.5
        LA = 2
        cm, cm_r = A.alloc("cm", [NQB, 4, P], BF16)
        pk_i, pk_ir = A.alloc("pk_i", [NB], I32)
        pk, pk_r = A.alloc("pk", [NB], F32)
        pqb, pqb_r = A.alloc("pqb", [TQ], F32)
        dma("sp", lambda e: e.dma_start(out=pk_i, in_=pos_allT), writes=[pk_ir])
        op("dve", lambda e: e.tensor_copy(pk, pk_i), reads=[pk_ir], writes=[pk_r])
        dma("sp", lambda e: e.dma_start(out=pqb, in_=posf_own.partition_broadcast(P)), writes=[pqb_r])
        for qb in range(NQB):
            for k in range(4):
                op("dve", lambda e: e.tensor_scalar(cm[:, qb, k, :], pqb[:, qb * P:(qb + 1) * P],
                                                    pk[:, 4 * qb + k:4 * qb + k + 1], None, ALU.is_ge),
                   reads=[pqb_r, pk_r], writes=[cm_r])
        sets = []
        for i in range(2):
            Kn, Kn_r = A.alloc(f"Kn{i}", [S], BF16)
            Kr, Kr_r = A.alloc(f"Kr{i}", [S], BF16)
            V, V_r = A.alloc(f"V{i}", [NB, 130], BF16)
            qn, qn_r = A.alloc(f"qn{i}", [TQ], BF16)
            qr, qr_r = A.alloc(f"qr{i}", [TQ], BF16)
            op("dve", lambda e: e.memset(Kr, 0.0), writes=[Kr_r])
            op("dve", lambda e: e.memset(qr, 0.0), writes=[qr_r])
            sets.append((Kn, Kn_r, Kr, Kr_r, V, V_r, qn, qn_r, qr, qr_r))
        NE = 4
        es_ = [A.alloc(f"e{i}", [4, P], BF16) for i in range(NE)]
        rden, rden_r = A.alloc("rden", [4], F32)
        yts = [A.alloc(f"yt{i}", [P], BF16) for i in range(2)]
        yos = [A.alloc(f"yo{i}", [P], BF16) for i in range(2)]
        sbank = (0, 1, 2)
        abank = (3, 4, 5, 6)
        steps = []
        for h in range(c.HB):
            for i0 in range(0, NQB, 4):
                for sb in range(4 * i0 + 16):
                    steps.append((h, i0, sb))
        epi = [0]

        def load_head(h):
            Kn, Kn_r, Kr, Kr_r, V, V_r, qn, qn_r, qr, qr_r = sets[h % 2]
            dma("sp", lambda e: e.dma_start(out=Kn, in_=KnT_all[h]), writes=[Kn_r])
            dma("sp", lambda e: e.dma_start(out=Kr[0:64, :], in_=KrT_all[h]), writes=[Kr_r])
            dma("sp", lambda e: e.dma_start(out=V, in_=V_all[h]), writes=[V_r])
            dma("sp", lambda e: e.dma_start(out=qn, in_=qbnT[h]), writes=[qn_r])
            dma("sp", lambda e: e.dma_start(out=qr[0:64, :], in_=qbrT[h]), writes=[qr_r])

        def stage1(i):
            h, i0, sb = steps[i]
            if i0 == 0 and sb == 0:
                if h == 0:
                    load_head(0)
                if h + 1 < c.HB:
                    load_head(h + 1)
            Kn, Kn_r, Kr, Kr_r, V, V_r, qn, qn_r, qr, qr_r = sets[h % 2]
            psi = sbank[i % 3]
            e_, e_r = es_[i % NE]
            mm(PS(psi), Kn[:, sb * P:(sb + 1) * P], qn[:, i0 * P:(i0 + 4) * P], True, False, [Kn_r, qn_r], [psr[psi]])
            mm(PS(psi), Kr[:, sb * P:(sb + 1) * P], qr[:, i0 * P:(i0 + 4) * P], False, True, [Kr_r, qr_r], [psr[psi]])
            op("act", lambda e: e.activation(e_.rearrange("p k t -> p (k t)"), PS(psi), AF.Exp, scale=scaleB),
               reads=[psr[psi]], writes=[e_r])
            for q in range(4):
                qi_ = i0 + q
                if 4 * qi_ <= sb < 4 * qi_ + 4:
                    op("dve", lambda e: e.tensor_tensor(e_[:, q, :], e_[:, q, :], cm[:, qi_, sb - 4 * qi_, :], ALU.mult),
                       reads=[e_r, cm_r], writes=[e_r])

        def stage2(i):
            h, i0, sb = steps[i]
            Kn, Kn_r, Kr, Kr_r, V, V_r, qn, qn_r, qr, qr_r = sets[h % 2]
            e_, e_r = es_[i % NE]
            for q in range(4):
                qi_ = i0 + q
                if sb >= 4 * qi_ + 4:
                    continue
                ab = abank[q]
                last = sb == 4 * qi_ + 3
                mm(PS(ab)[:, 0:130], e_[:, q, :], V[:, sb, :], sb == 0, last, [e_r, V_r], [psr[ab]])
                if last:
                    yt, yt_r = yts[epi[0] % 2]
                    yo, yo_r = yos[epi[0] % 2]
                    epi[0] += 1
                    op("dve", lambda e: e.reciprocal(rden[:, q:q + 1], PS(ab)[:, 128:129]), reads=[psr[ab]], writes=[rden_r])
                    op("dve", lambda e: e.tensor_scalar(yt, PS(ab)[:, 0:128], rden[:, q:q + 1], None, ALU.mult),
                       reads=[psr[ab], rden_r], writes=[yt_r])
                    mm(PS(7)[:, 0:P], yt, ident, True, True, [yt_r, ident_r], [psr[7]])
                    op("act", lambda e: e.activation(yo, PS(7)[:, 0:P], AF.Copy), reads=[psr[7]], writes=[yo_r])
                    dma("sp", lambda e: e.dma_start(out=ybT[h, :, qi_ * P:(qi_ + 1) * P], in_=yo), reads=[yo_r])
        n = len(steps)
        for i in range(n + LA):
            if i < n:
                stage1(i)
            if i >= LA:
                stage2(i - LA)
        A.release()
        S_.barrier()

    def phase6():
        A.mark()
        HAc = c.HA
        HBc = c.HB
        X, X_r = A.alloc("X", [DC, TG], F32)
        rstd, rstd_r = A.alloc("rstd", [TG], F32)
        tmp, tmp_r = A.alloc("tmp", [TG], F32)
        ts_ = [A.alloc(f"t{i}", [TG], F32) for i in range(4)]
        WCOL = 256
        KMAX = max(DC, HAc, HBc, c.FC)
        WELEMS = max(KMAX * WCOL, max(DC, HAc, HBc) * 512)
        Ws = [A.alloc(f"W{i}", [WELEMS], BF16) for i in range(3)]
        wi_ = [0]

        def load_w(nm, kch, col0, ncol):
            Wf, W_r = Ws[wi_[0] % 3]
            wi_[0] += 1
            W = Wf[:, 0:kch * ncol].rearrange("p (k n) -> p k n", n=ncol)
            dma("sp", lambda e: e.dma_start(out=W, in_=wb[nm][0].rearrange("(c p) n -> p c n", p=P)[:, :, col0:col0 + ncol]),
                reads=[wb[nm][2]], writes=[W_r])
            return W, W_r
        ti = [0]

        def nxt_t():
            ti[0] += 1
            return ts_[ti[0] % 4]
        pctr = [0]

        def nxt_ps():
            pctr[0] += 1
            return 1 + (pctr[0] % 4)
        NM = WCOL // P
        WC2 = 512 if (D % 512 == 0 and c.DFF % 512 == 0) else 256
        NM2 = WC2 // P
        for g in range(NQG):
            t0 = g * TG
            load_x_group(xT_own, t0, X, X_r)
            A.mark()
            ya, ya_r = A.alloc("ya", [HAc, TG], BF16)
            yb, yb_r = A.alloc("yb", [HBc, TG], BF16)
            gts = [A.alloc(f"gt{i}", [2, NM2, TG], BF16) for i in range(2)]
            mx, mx_r = A.alloc("mx", [DC, TG], BF16)
            dma("sp", lambda e: e.dma_start(out=ya, in_=yaT[:, :, t0:t0 + TG].rearrange("h p t -> p h t")), writes=[ya_r])
            dma("sp", lambda e: e.dma_start(out=yb, in_=ybT[:, :, t0:t0 + TG].rearrange("h p t -> p h t")), writes=[yb_r])
            for ni, n0 in enumerate(range(0, D, WC2)):
                Wa, Wa_r = load_w("w_oa", HAc, n0, WC2)
                Wb_, Wb_r = load_w("w_ob", HBc, n0, WC2)
                gt, gt_r = gts[ni % 2]
                nb0 = n0 // P
                for br in range(2):
                    dma("sp", lambda e, gt=gt, br=br, nb0=nb0: e.dma_start(
                        out=gt[:, br], in_=gatesT[br * DC + nb0:br * DC + nb0 + NM2, :, t0:t0 + TG].rearrange("h p t -> p h t")),
                        writes=[gt_r])
                for m in range(NM2):
                    n = nb0 + m
                    pa, pb = nxt_ps(), nxt_ps()
                    for k in range(HAc):
                        mm(PS(pa), Wa[:, k, m * P:(m + 1) * P], ya[:, k, :], k == 0, k == HAc - 1, [Wa_r, ya_r], [psr[pa]])
                    for k in range(HBc):
                        mm(PS(pb), Wb_[:, k, m * P:(m + 1) * P], yb[:, k, :], k == 0, k == HBc - 1, [Wb_r, yb_r], [psr[pb]])
                    ta, ta_r = nxt_t()
                    tb_, tb_r = nxt_t()
                    op("dve", lambda e, ta=ta, pa=pa, gt=gt, m=m: e.tensor_tensor(ta, PS(pa), gt[:, 0, m, :], ALU.mult),
                       reads=[psr[pa], gt_r], writes=[ta_r])
                    op("dve", lambda e, tb_=tb_, pb=pb, gt=gt, m=m: e.tensor_tensor(tb_, PS(pb), gt[:, 1, m, :], ALU.mult),
                       reads=[psr[pb], gt_r], writes=[tb_r])
                    op("pool", lambda e, ta=ta, tb_=tb_, n=n: e.tensor_tensor(mx[:, n, :], ta, tb_, ALU.add),
                       reads=[ta_r, tb_r], writes=[mx_r])
            for n0 in range(0, D, WC2):
                W, W_r = load_w("w_o", DC, n0, WC2)
                for m in range(NM2):
                    n = (n0 // P) + m
                    pa = nxt_ps()
                    for k in range(DC):
                        mm(PS(pa), W[:, k, m * P:(m + 1) * P], mx[:, k, :], k == 0, k == DC - 1, [W_r, mx_r], [psr[pa]])
                    op("dve", lambda e, pa=pa, n=n: e.tensor_tensor(X[:, n, :], X[:, n, :], PS(pa), ALU.add),
                       reads=[psr[pa], X_r], writes=[X_r])
            S_.barrier()
            A.release()
            A.mark()
            hg, hg_r = A.alloc("hg", [DC, TG], BF16)
            act, act_r = A.alloc("act", [max(c.FC, DC), TG], BF16)
            sq = act[:, 0:DC, :]
            prep_x(X, X_r, "g_ffn", sq, act_r, hg, hg_r, rstd, rstd_r, tmp, tmp_r, 0)
            for n0 in range(0, c.DFF, WC2):
                Wg, Wg_r = load_w("w_fg", DC, n0, WC2)
                Wu, Wu_r = load_w("w_fu", DC, n0, WC2)
                for m in range(NM2):
                    n = (n0 // P) + m
                    pa, pb = nxt_ps(), nxt_ps()
                    for k in range(DC):
                        mm(PS(pa), Wg[:, k, m * P:(m + 1) * P], hg[:, k, :], k == 0, k == DC - 1, [Wg_r, hg_r], [psr[pa]])
                    for k in range(DC):
                        mm(PS(pb), Wu[:, k, m * P:(m + 1) * P], hg[:, k, :], k == 0, k == DC - 1, [Wu_r, hg_r], [psr[pb]])
                    ta, ta_r = nxt_t()
                    tb_, tb_r = nxt_t()
                    tc, tc_r = nxt_t()
                    op("dve", lambda e, ta=ta, pa=pa: e.tensor_tensor(ta, PS(pa), rstd, ALU.mult),
                       reads=[psr[pa], rstd_r], writes=[ta_r])
                    op("act", lambda e, ta=ta, tc=tc: e.activation(tc, ta, AF.Silu), reads=[ta_r], writes=[tc_r])
                    op("dve", lambda e, tb_=tb_, pb=pb: e.tensor_tensor(tb_, PS(pb), rstd, ALU.mult),
                       reads=[psr[pb], rstd_r], writes=[tb_r])
                    op("pool", lambda e, tc=tc, tb_=tb_, n=n: e.tensor_tensor(act[:, n, :], tc, tb_, ALU.mult),
                       reads=[tc_r, tb_r], writes=[act_r])
            for n0 in range(0, D, WCOL):
                W, W_r = load_w("w_fd", c.FC, n0, WCOL)
                for m in range(NM):
                    n = (n0 // P) + m
                    pa = nxt_ps()
                    for k in range(c.FC):
                        mm(PS(pa), W[:, k, m * P:(m + 1) * P], act[:, k, :], k == 0, k == c.FC - 1, [W_r, act_r], [psr[pa]])
                    op("dve", lambda e, pa=pa, n=n: e.tensor_tensor(X[:, n, :], X[:, n, :], PS(pa), ALU.add),
                       reads=[psr[pa], X_r], writes=[X_r])
            S_.barrier()
            A.release()
            A.mark()
            hg, hg_r = A.alloc("hg", [DC, TG], BF16)
            sq, sq_r = A.alloc("sq", [DC, TG], BF16)
            pt, pt_r = A.alloc("pt", [c.PC, TG], BF16)
            oo = [A.alloc(f"oo{i}", [TG], F32) for i in range(2)]
            dma("sp", lambda e: e.dma_start(out=pt, in_=wb["pT"][0].rearrange("(c p) t -> p c t", p=P)[:, :, t0:t0 + TG]),
                reads=[wb["pT"][2]], writes=[pt_r])
            prep_x(X, X_r, "g_ple", sq, sq_r, hg, hg_r, rstd, rstd_r, tmp, tmp_r, 0)
            for n0 in range(0, D, WC2):
                Wg, Wg_r = load_w("w_pg", DC, n0, WC2)
                Wp, Wp_r = load_w("w_pp", c.PC, n0, WC2)
                for m in range(NM2):
                    n = (n0 // P) + m
                    pa, pb = nxt_ps(), nxt_ps()
                    for k in range(DC):
                        mm(PS(pa), Wg[:, k, m * P:(m + 1) * P], hg[:, k, :], k == 0, k == DC - 1, [Wg_r, hg_r], [psr[pa]])
                    for k in range(c.PC):
                        mm(PS(pb), Wp[:, k, m * P:(m + 1) * P], pt[:, k, :], k == 0, k == c.PC - 1, [Wp_r, pt_r], [psr[pb]])
                    ta, ta_r = nxt_t()
                    tc, tc_r = nxt_t()
                    o_, o_r = oo[n % 2]
                    op("dve", lambda e, ta=ta, pa=pa: e.tensor_tensor(ta, PS(pa), rstd, ALU.mult),
                       reads=[psr[pa], rstd_r], writes=[ta_r])
                    op("act", lambda e, ta=ta, tc=tc: e.activation(tc, ta, AF.Sigmoid), reads=[ta_r], writes=[tc_r])
                    op("dve", lambda e, tc=tc, pb=pb: e.tensor_tensor(tc, tc, PS(pb), ALU.mult),
                       reads=[tc_r, psr[pb]], writes=[tc_r])
                    op("pool", lambda e, o_=o_, tc=tc, n=n: e.tensor_tensor(o_, tc, X[:, n, :], ALU.add),
                       reads=[tc_r, X_r], writes=[o_r])
                    dma("sp", lambda e, o_=o_, n=n: e.dma_start(out=outT[n * P:(n + 1) * P, t0:t0 + TG], in_=o_), reads=[o_r])
            S_.barrier()
            A.release()
        A.release()

    phase1()
    phase2()
    phase3()
    phase4()
    phase5a()
    phase5b()
    phase6()
    S_.final_wait()

    sems = {}
    for k in S_.cnt:
        sems[k] = es.enter_context(nc.semaphore(k))
    with nc.Block() as block:
        block.sync(lambda e: S_.replay("sp", e, sems))
        block.tensor(lambda e: S_.replay("pe", e, sems))
        block.scalar(lambda e: S_.replay("act", e, sems))
        block.vector(lambda e: S_.replay("dve", e, sems))
        block.gpsimd(lambda e: S_.replay("pool", e, sems))
    es.close()
    return nc


def _perm_matrix(blocks, half):
    m = np.zeros((P, P), np.float32)
    bs = 2 * half
    for b in range(blocks):
        base = b * bs
        for d in range(half):
            m[base + d + half, base + d] = -1.0
            m[base + d, base + d + half] = 1.0
    return m


def host_consts(cfg, g):
    lay, ncc = _consts_layout(cfg)
    cst = np.zeros((P, ncc), np.float32)

    def put(name, arr):
        o, w = lay[name]
        a = np.asarray(arr, np.float32)
        if a.ndim == 1:
            a = a[:, None]
        cst[0:a.shape[0], o:o + w] = a
    fA = np.power(np.float32(THETA), -np.arange(64, dtype=np.float32) * np.float32(2.0 / 128)).astype(np.float32)
    fI = np.power(np.float32(THETA), -np.arange(32, dtype=np.float32) * np.float32(2.0 / 64)).astype(np.float32)
    put("freqA", np.concatenate([fA, fA]))
    put("freqI", np.concatenate([fI, fI, fI, fI]))

    def chunked(v):
        return np.asarray(v, np.float32).reshape(-1, P).T
    put("g_mix", chunked(g["g_mix_norm"]))
    put("g_qa", g["g_qa"])
    put("g_ka", g["g_ka"])
    put("g_cq", chunked(g["g_cq"]))
    put("g_ckv", chunked(g["g_ckv"]))
    put("g_qbn", g["g_qb"][:128])
    put("g_qbr", g["g_qb"][128:])
    put("g_kbn", g["g_kb"][:128])
    put("g_kbr", g["g_kb"][128:])
    put("g_ffn", chunked(g["g_ffn_norm"]))
    put("g_ple", chunked(g["g_ple_norm"]))
    return cst


def host_inputs(cfg, inputs, core):
    c = cfg
    b, j = core // 4, core % 4
    sq = {k: np.asarray(v)[0] for k, v in inputs.items() if k not in ("x", "p", "positions")}
    x = np.asarray(inputs["x"], np.float32)[b]
    p = np.asarray(inputs["p"], np.float32)[0, b]
    pos = np.asarray(inputs["positions"], np.int32)[b]
    own = np.concatenate([np.arange((4 * i + j) * P, (4 * i + j + 1) * P) for i in range(c.NQB)])
    xT = np.ascontiguousarray(x.T)
    m = {
        "xT_all": xT,
        "xT_own": np.ascontiguousarray(xT[:, own]),
        "pT_own": np.ascontiguousarray(p[own].T),
        "pos_all": np.ascontiguousarray(pos[None, :]),
        "pos_own": np.ascontiguousarray(pos[own][None, :]),
        "pos_allT": np.ascontiguousarray(pos.reshape(c.NB, P).T),
        "pos_ownT": np.ascontiguousarray(pos[own].reshape(c.NQB, P).T),
        "consts": host_consts(c, sq),
        "ident": np.eye(P, dtype=np.float32).astype(ml_dtypes.bfloat16),
        "ones": np.ones((P, P), np.float32).astype(ml_dtypes.bfloat16),
        "permA": _perm_matrix(1, 64).astype(ml_dtypes.bfloat16),
        "permI": _perm_matrix(2, 32).astype(ml_dtypes.bfloat16),
        "identf": np.eye(P, dtype=np.float32),
    }
    for k in ("w_in", "w_uq", "w_ukv", "w_out_a", "w_out_b", "w_o", "w_ffn_gate", "w_ffn_up", "w_ffn_down",
              "w_ple_gate", "w_ple_proj"):
        m[k] = np.ascontiguousarray(sq[k], dtype=np.float32)
    return m, own


def kernel(**inputs):
    cfg = Cfg()
    nc = build_program(cfg)
    in_maps = []
    owns = []
    for core in range(8):
        m, own = host_inputs(cfg, inputs, core)
        in_maps.append(m)
        owns.append(own)
    res = run_bass_kernel_spmd(nc, in_maps, core_ids=list(range(8)))
    out = np.zeros((2, cfg.S, cfg.D), np.float32)
    for core in range(8):
        b = core // 4
        out[b, owns[core], :] = np.asarray(res.results[core]["outT"], np.float32).T
    return out
```
